# Optimizing a Trainium2 kernel written in Bass

```python
import jax, jax.numpy as jnp
from jax import lax
import numpy as np

D_MODEL = 1024
BATCH = 4
SEQ = 4096
DEPTH = 1

RET_HEADS = 8
RET_DK = 64
RET_DV = 128
DN_HEADS = 8
DN_DK = 128
DN_DV = 128
CHUNK = 128
SHORT_CONV = 4
FFN_CONV = 3
D_FF = 2816
ROPE_BASE = 10000.0
EPS = 1e-6
GN_EPS = 1e-5

RET_QK = RET_HEADS * RET_DK
RET_V = RET_HEADS * RET_DV
DN_QK = DN_HEADS * DN_DK
DN_V = DN_HEADS * DN_DV
SPLITS = (RET_QK, RET_QK, RET_V, RET_V, DN_QK, DN_QK, DN_V, DN_V, DN_HEADS, DN_HEADS, D_MODEL, D_MODEL)
D_IN = sum(SPLITS)
DN_CONV_CH = 2 * DN_QK + DN_V

kernel_name = "retention_gated_deltanet_convffn_hybrid"


def rmsnorm(x, g):
    xf = x.astype(jnp.float32)
    r = lax.rsqrt(jnp.mean(xf * xf, axis=-1, keepdims=True) + EPS)
    return (xf * r) * g


def l2norm(x):
    xf = x.astype(jnp.float32)
    return xf * lax.rsqrt(jnp.sum(xf * xf, axis=-1, keepdims=True) + EPS)


def rotary(x):
    T, d = x.shape[1], x.shape[-1]
    inv = ROPE_BASE ** (-jnp.arange(0, d, 2, dtype=jnp.float32) / d)
    ang = jnp.arange(T, dtype=jnp.float32)[:, None] * inv[None, :]
    cos = jnp.cos(ang)[None, :, None, :]
    sin = jnp.sin(ang)[None, :, None, :]
    x1, x2 = x[..., : d // 2], x[..., d // 2:]
    return jnp.concatenate([x1 * cos - x2 * sin, x1 * sin + x2 * cos], axis=-1)


def causal_dwconv(x, w):
    K, T = w.shape[0], x.shape[1]
    xp = jnp.pad(x, ((0, 0), (K - 1, 0), (0, 0)))
    y = xp[:, K - 1:K - 1 + T] * w[K - 1]
    for k in range(K - 1):
        y = y + xp[:, k:k + T] * w[k]
    return y


def retention_chunkwise(q, k, v):
    B, T, H, dk = q.shape
    dv = v.shape[-1]
    N = T // CHUNK
    f32 = jnp.float32
    gamma = 1.0 - 2.0 ** (-5.0 - jnp.arange(H, dtype=f32))
    log_g = jnp.log(gamma)
    idx = jnp.arange(CHUNK, dtype=f32)
    qc = q.astype(f32).reshape(B, N, CHUNK, H, dk)
    kc = k.astype(f32).reshape(B, N, CHUNK, H, dk)
    vc = v.astype(f32).reshape(B, N, CHUNK, H, dv)
    diff = idx[:, None] - idx[None, :]
    causal = diff >= 0
    dmat = jnp.where(causal[None], jnp.exp(log_g[:, None, None] * jnp.where(causal, diff, 0.0)[None]), 0.0)
    scores = jnp.einsum('bnihd,bnjhd->bnhij', qc, kc) * dmat
    inner = jnp.einsum('bnhij,bnjhv->bnihv', scores, vc)
    zeta = jnp.exp(log_g[:, None] * (CHUNK - 1.0 - idx)[None, :])
    kv = jnp.einsum('bnjhd,hj,bnjhv->bnhdv', kc, zeta, vc)
    chunk_decay = jnp.exp(log_g * CHUNK)[:, None, None]

    def step(S, kv_n):
        return S * chunk_decay + kv_n, S

    _, s_prev = lax.scan(step, jnp.zeros((B, H, dk, dv), f32), jnp.moveaxis(kv, 1, 0))
    xi = jnp.exp(log_g[:, None] * (idx + 1.0)[None, :])
    cross = jnp.einsum('bnihd,nbhdv,hi->bnihv', qc, s_prev, xi)
    return (inner + cross).reshape(B, T, H, dv)


def gated_delta_chunkwise(q, k, v, g, beta):
    B, T, H, dk = q.shape
    dv = v.shape[-1]
    N = T // CHUNK
    f32 = jnp.float32

    def to_chunks(t):
        return jnp.moveaxis(t.astype(f32).reshape((B, N, CHUNK, H) + t.shape[3:]), 3, 1)

    qc = to_chunks(q) * (dk ** -0.5)
    kc = to_chunks(k)
    vc = to_chunks(v)
    gc = to_chunks(g)
    bc = to_chunks(beta)
    G = jnp.cumsum(gc, axis=-1)
    idx = jnp.arange(CHUNK)
    causal = idx[:, None] >= idx[None, :]
    strict = idx[:, None] > idx[None, :]
    diff = G[..., :, None] - G[..., None, :]
    L = jnp.where(causal, jnp.exp(jnp.where(causal, diff, 0.0)), 0.0)
    k_beta = kc * bc[..., None]
    A = jnp.where(strict, jnp.einsum('bhnid,bhnjd->bhnij', k_beta, kc) * L, 0.0)
    eye = jnp.eye(CHUNK, dtype=f32)
    rhs = jnp.concatenate([vc * bc[..., None], k_beta * jnp.exp(G)[..., None]], axis=-1)
    sol = lax.linalg.triangular_solve(eye + A, rhs, left_side=True, lower=True, unit_diagonal=True)
    u, w = sol[..., :dv], sol[..., dv:]
    attn = jnp.where(causal, jnp.einsum('bhnid,bhnjd->bhnij', qc, kc) * L, 0.0)
    q_dec = qc * jnp.exp(G)[..., None]
    k_dec = kc * jnp.exp(G[..., -1:] - G)[..., None]
    chunk_dec = jnp.exp(G[..., -1])

    def step(S, inp):
        u_n, w_n, q_n, k_n, a_n, d_n = inp
        v_new = u_n - jnp.einsum('bhcd,bhdv->bhcv', w_n, S)
        o = jnp.einsum('bhcd,bhdv->bhcv', q_n, S) + jnp.einsum('bhij,bhjv->bhiv', a_n, v_new)
        S = S * d_n[..., None, None] + jnp.einsum('bhcd,bhcv->bhdv', k_n, v_new)
        return S, o

    xs = tuple(jnp.moveaxis(t, 2, 0) for t in (u, w, q_dec, k_dec, attn, chunk_dec))
    _, o = lax.scan(step, jnp.zeros((B, H, dk, dv), f32), xs)
    return o.transpose(1, 0, 3, 2, 4).reshape(B, T, H, dv)


def setup_inputs(seed: int = 0) -> dict:
    key = jax.random.key(seed)
    ks = jax.random.split(key, 20)
    f32 = jnp.float32
    nrm = lambda k, shape, s: jax.random.normal(k, shape, f32) * s
    dt = jnp.exp(jax.random.uniform(ks[6], (DEPTH, DN_HEADS), f32, np.log(1e-3), np.log(1e-1)))
    return {
        "x": jax.random.normal(ks[0], (BATCH, SEQ, D_MODEL), f32),
        "g_mix": 1.0 + nrm(ks[1], (DEPTH, D_MODEL), 0.05),
        "w_in": nrm(ks[2], (DEPTH, D_MODEL, D_IN), D_MODEL ** -0.5),
        "ret_norm_g": 1.0 + nrm(ks[3], (DEPTH, RET_V), 0.05),
        "dn_conv_w": nrm(ks[4], (DEPTH, SHORT_CONV, DN_CONV_CH), SHORT_CONV ** -0.5),
        "dn_a_log": jnp.log(jax.random.uniform(ks[5], (DEPTH, DN_HEADS), f32, 1.0, 16.0)),
        "dn_dt_bias": dt + jnp.log(-jnp.expm1(-dt)),
        "dn_norm_g": 1.0 + nrm(ks[7], (DEPTH, DN_DV), 0.05),
        "w_ret_br": nrm(ks[8], (DEPTH, RET_V, D_MODEL), RET_V ** -0.5),
        "w_dn_br": nrm(ks[9], (DEPTH, DN_V, D_MODEL), DN_V ** -0.5),
        "w_o": nrm(ks[10], (DEPTH, D_MODEL, D_MODEL), D_MODEL ** -0.5),
        "g_ffn": 1.0 + nrm(ks[11], (DEPTH, D_MODEL), 0.05),
        "w_up": nrm(ks[12], (DEPTH, D_MODEL, 2 * D_FF), D_MODEL ** -0.5),
        "ffn_conv_w": nrm(ks[13], (DEPTH, FFN_CONV, 2 * D_FF), FFN_CONV ** -0.5),
        "ffn_conv_b": nrm(ks[14], (DEPTH, 2 * D_FF), 0.01),
        "w_down": nrm(ks[15], (DEPTH, D_FF, D_MODEL), D_FF ** -0.5),
        "g_final": 1.0 + nrm(ks[16], (D_MODEL,), 0.05),
    }


def reference(x, g_mix, w_in, ret_norm_g, dn_conv_w, dn_a_log, dn_dt_bias, dn_norm_g,
              w_ret_br, w_dn_br, w_o, g_ffn, w_up, ffn_conv_w, ffn_conv_b, w_down, g_final):
    B, T, _ = x.shape
    h = x.astype(jnp.float32)
    split_at = np.cumsum(SPLITS)[:-1].tolist()
    for l in range(DEPTH):
        u = rmsnorm(h, g_mix[l])
        proj = u @ w_in[l]
        (rq, rk, rv, rgate, dq, dk_, dv_, dz, db, da, gate_r, gate_d) = jnp.split(proj, split_at, axis=-1)

        rq = rotary(rq.reshape(B, T, RET_HEADS, RET_DK))
        rk = rotary(rk.reshape(B, T, RET_HEADS, RET_DK)) * (RET_DK ** -0.5)
        ro = retention_chunkwise(rq, rk, rv.reshape(B, T, RET_HEADS, RET_DV))
        mu = jnp.mean(ro, axis=-1, keepdims=True)
        var = jnp.mean(jnp.square(ro - mu), axis=-1, keepdims=True)
        ro = ((ro - mu) * lax.rsqrt(var + GN_EPS)).reshape(B, T, RET_V) * ret_norm_g[l]
        y_ret = (jax.nn.silu(rgate) * ro) @ w_ret_br[l]

        qkv = jax.nn.silu(causal_dwconv(jnp.concatenate([dq, dk_, dv_], axis=-1), dn_conv_w[l]))
        cq, ck, cv = jnp.split(qkv, [DN_QK, 2 * DN_QK], axis=-1)
        cq = l2norm(cq.reshape(B, T, DN_HEADS, DN_DK))
        ck = l2norm(ck.reshape(B, T, DN_HEADS, DN_DK))
        cv = cv.reshape(B, T, DN_HEADS, DN_DV)
        g = -jnp.exp(dn_a_log[l]) * jax.nn.softplus(da.astype(jnp.float32) + dn_dt_bias[l])
        beta = jax.nn.sigmoid(db.astype(jnp.float32))
        do = gated_delta_chunkwise(cq, ck, cv, g, beta)
        do = do * lax.rsqrt(jnp.mean(do * do, axis=-1, keepdims=True) + EPS) * dn_norm_g[l]
        do = do.reshape(B, T, DN_V) * jax.nn.silu(dz)
        y_dn = do @ w_dn_br[l]

        merged = jax.nn.sigmoid(gate_r) * y_ret + jax.nn.sigmoid(gate_d) * y_dn
        h = h + merged @ w_o[l]

        u2 = rmsnorm(h, g_ffn[l])
        up = causal_dwconv(u2 @ w_up[l], ffn_conv_w[l]) + ffn_conv_b[l]
        a, b = jnp.split(up, 2, axis=-1)
        h = h + (jax.nn.silu(a) * b) @ w_down[l]
    return rmsnorm(h, g_final).astype(x.dtype)
```

```python
import numpy as np
from contextlib import ExitStack
import ml_dtypes
import concourse.bass as bass
import concourse.mybir as mybir
from concourse.bass_utils import run_bass_kernel_spmd

F32 = mybir.dt.float32
BF16 = mybir.dt.bfloat16
AF = mybir.ActivationFunctionType
ALU = mybir.AluOpType
AX = mybir.AxisListType

D = 1024
DFF = 2816
NFC = 44
EPS = 1e-6
GN_EPS = 1e-5


class Buf:
    def __init__(self, name):
        self.name = name
        self.last_write = None
        self.reads = {}
        self.dsem = None
        self.dcount = 0


class EngState:
    def __init__(self, name, eng, sem):
        self.name = name
        self.eng = eng
        self.sem = sem
        self.count = 0
        self.waited = {}
        self.pend_r = []
        self.pend_w = []


class FW:
    def __init__(self, nc, stack):
        self.nc = nc
        self.stack = stack
        self.engs = {}
        for name, eng in (("pe", nc.tensor), ("act", nc.scalar), ("dve", nc.vector),
                          ("pool", nc.gpsimd), ("sp", nc.sync)):
            sem = stack.enter_context(nc.semaphore(name + "_sem"))
            self.engs[name] = EngState(name, eng, sem)
        self.nwaits = 0
        self.ninst = 0
        self.uid = 0
        self.dma_bufs = []
        import os as _os, sys as _sys
        self.limit = int(_os.environ.get("FW_LIMIT", "0")) or None
        self.lines = []
        self._sys = _sys

    def _skip(self):
        self.lines.append(self._sys._getframe(2).f_lineno if self.limit is not None or True else 0)
        return self.limit is not None and self.ninst >= self.limit

    def sbuf(self, st, name, shape, dtype):
        t = st.enter_context(self.nc.sbuf_tensor("s_" + name, list(shape), dtype))
        return t, Buf(name)

    def psum(self, st, name, shape, dtype):
        t = st.enter_context(self.nc.psum_tensor(name, list(shape), dtype))
        return t

    def _wait(self, E, tok):
        if tok is None:
            return
        sem, val = tok
        k = id(sem)
        if E.waited.get(k, 0) >= val:
            return
        E.eng.wait_ge(sem, val)
        E.waited[k] = val
        self.nwaits += 1

    def _deps(self, E, reads, writes, pe):
        for b in reads:
            t = b.last_write
            if t is not None and not (pe and t[0] is E.sem):
                self._wait(E, t)
            if getattr(b, "psum", False):
                for t in list(b.reads.values()):
                    if t[0] is not E.sem:
                        self._wait(E, t)
        for b in writes:
            t = b.last_write
            if t is not None and not (pe and t[0] is E.sem):
                self._wait(E, t)
            for t in list(b.reads.values()):
                if not (pe and t[0] is E.sem):
                    self._wait(E, t)

    def _commit(self, tok, reads, writes):
        for b in writes:
            b.last_write = tok
            b.reads = {}
        for b in reads:
            if b.last_write is not tok:
                b.reads[id(tok[0])] = tok

    def op(self, en, fn, reads=(), writes=(), inc=True):
        if self._skip():
            return None
        E = self.engs[en]
        self._deps(E, reads, writes, en == "pe")
        ins = fn(E.eng)
        self.ninst += 1
        E.pend_r.extend(reads)
        E.pend_w.extend(writes)
        if inc:
            E.count += 1
            ins.then_inc(E.sem, 1)
            tok = (E.sem, E.count)
            self._commit(tok, E.pend_r, E.pend_w)
            E.pend_r = []
            E.pend_w = []
        return ins

    def dma(self, qn, out, in_, reads=(), writes=(), **kw):
        if self._skip():
            return None
        E = self.engs[qn]
        self._deps(E, reads, writes, False)
        owner = writes[0] if writes else reads[0]
        if owner.dsem is None:
            self.uid += 1
            owner.dsem = self.stack.enter_context(self.nc.semaphore("d%d_%s" % (self.uid, owner.name)))
            self.dma_bufs.append(owner)
        owner.dcount += 1
        ins = E.eng.dma_start(out=out, in_=in_, **kw)
        ins.then_inc(owner.dsem, 16)
        tok = (owner.dsem, 16 * owner.dcount)
        self._commit(tok, list(reads), list(writes))
        self.ninst += 1
        return ins

    def barrier(self):
        for E in self.engs.values():
            for F in self.engs.values():
                if F is not E and F.count > 0:
                    self._wait(E, (F.sem, F.count))
            for b in self.dma_bufs:
                self._wait(E, (b.dsem, 16 * b.dcount))

    def finish(self, bufs):
        E = self.engs["sp"]
        for b in bufs:
            self._wait(E, b.last_write)
            for t in b.reads.values():
                self._wait(E, t)


def bf16_np(a):
    return np.ascontiguousarray(a).astype(ml_dtypes.bfloat16)


def load_cast_weight(fw, stg, dst, bdst, src, nk, ncols, scale=None, bscale=None, qn="sp"):
    CW = 1024
    n = 0
    for kd in range(nk):
        for c0 in range(0, ncols, CW):
            cw = min(CW, ncols - c0)
            s, bs = stg[n % 2]
            fw.dma(qn, s[:, 0:cw], src[kd * 128:(kd + 1) * 128, c0:c0 + cw], writes=[bs])
            if scale is not None:
                fw.op("act", lambda e, s=s, kd=kd, c0=c0, cw=cw: e.activation(
                    out=dst[:, kd, c0:c0 + cw], in_=s[:, 0:cw], func=AF.Copy, scale=scale[:, kd:kd + 1]),
                    reads=[bs, bscale], writes=[bdst])
            else:
                en = "dve" if (n % 2 == 0) else "pool"
                fw.op(en, lambda e, s=s, kd=kd, c0=c0, cw=cw: e.tensor_copy(out=dst[:, kd, c0:c0 + cw], in_=s[:, 0:cw]),
                      reads=[bs], writes=[bdst])
            n += 1


def rms_stats(fw, ss, bss, tmp, btmp, rstd, brstd, n, inv_d, eps):
    fw.op("dve", lambda e: e.tensor_scalar(out=tmp[:, 0:n], in0=ss[:, 0:n], scalar1=inv_d, scalar2=eps,
                                           op0=ALU.mult, op1=ALU.add), reads=[bss], writes=[btmp])
    fw.op("act", lambda e: e.activation(out=tmp[:, 0:n], in_=tmp[:, 0:n], func=AF.Sqrt), reads=[btmp], writes=[btmp])
    fw.op("dve", lambda e: e.reciprocal(out=rstd[:, 0:n], in_=tmp[:, 0:n]), reads=[btmp], writes=[brstd])


def build_B(TB):
    NBLK = TB // 128 + 1
    nc = bass.Bass("TRN2", target_bir_lowering=False)
    dt = lambda name, shape, dty=F32, kind="ExternalInput": nc.dram_tensor(name, list(shape), dty, kind=kind).ap()
    xB = dt("xB", [NBLK * 128, D])
    ogT = dt("ogT", [2 * D, NBLK * 128], BF16)
    wg_d = dt("wg", [D, 2 * D])
    wrb_d = dt("wrb", [D, D])
    wdb_d = dt("wdb", [D, D])
    wo_d = dt("wo", [D, D])
    wup_d = dt("wup", [D, 2 * DFF])
    wdn_d = dt("wdn", [DFF, D])
    gmix_d = dt("gmix", [128, 8])
    gffn_d = dt("gffn", [128, 8])
    cw_d = dt("fcw", [128, NFC, 3])
    cb_d = dt("fcb", [128, NFC])
    gfin_d = dt("gfin", [128, D])
    ident_d = dt("ident", [128, 128])
    out_d = dt("out", [TB, D], F32, "ExternalOutput")
    hS = dt("hS", [NBLK * 128, D], F32, "Internal")
    bhS = Buf("hS")
    bout = Buf("out")

    with ExitStack() as st:
        fw = FW(nc, st)
        idf, bidf = fw.sbuf(st, "idf", [128, 128], F32)
        idb, bidb = fw.sbuf(st, "idb", [128, 128], BF16)
        gmix, bgmix = fw.sbuf(st, "gmix", [128, 8], F32)
        gffn, bgffn = fw.sbuf(st, "gffn", [128, 8], F32)
        fcw, bfcw = fw.sbuf(st, "fcw", [128, NFC, 3], F32)
        fcb, bfcb = fw.sbuf(st, "fcb", [128, NFC], F32)
        gfin, bgfin = fw.sbuf(st, "gfin", [128, D], F32)
        ss, bss = fw.sbuf(st, "ss", [128, 8], F32)
        tmpS, btmpS = fw.sbuf(st, "tmpS", [128, 8], F32)
        rstd, brstd = fw.sbuf(st, "rstd", [128, 8], F32)
        fw.dma("sp", idf[:], ident_d, writes=[bidf])
        fw.dma("sp", gmix[:], gmix_d, writes=[bgmix])
        fw.dma("sp", gffn[:], gffn_d, writes=[bgffn])
        fw.dma("sp", fcw[:], cw_d, writes=[bfcw])
        fw.dma("sp", fcb[:], cb_d, writes=[bfcb])
        fw.dma("sp", gfin[:], gfin_d, writes=[bgfin])
        fw.op("dve", lambda e: e.tensor_copy(out=idb[:], in_=idf[:]), reads=[bidf], writes=[bidb])
        banks = [fw.psum(st, "bank%d" % i, [128, 512], F32) for i in range(8)]
        bbank = [Buf("bank%d" % i) for i in range(8)]
        for b_ in bbank:
            b_.psum = True

        def norm_transpose(src, bsrc, nblk, xn_bufs, uT, buT, tbanks, sq_junk, bjunk):
            for blk in range(nblk):
                fw.op("act", lambda e, blk=blk: e.activation(out=sq_junk[:], in_=src[blk][:], func=AF.Square,
                                                             accum_out=ss[:, blk:blk + 1]),
                      reads=[bsrc[blk]], writes=[bjunk, bss])
            rms_stats(fw, ss, bss, tmpS, btmpS, rstd, brstd, nblk, 1.0 / D, EPS)
            for blk in range(nblk):
                xn, bxn = xn_bufs[blk % 2]
                fw.op("act", lambda e, blk=blk, xn=xn: e.activation(out=xn[:], in_=src[blk][:], func=AF.Copy,
                                                                   scale=rstd[:, blk:blk + 1]),
                      reads=[bsrc[blk], brstd], writes=[bxn])
                tb = tbanks[blk % 2]
                pT = banks[tb][:].bitcast(BF16).rearrange("p (k t) -> p k t", k=8)
                for kd in range(8):
                    fw.op("pe", lambda e, kd=kd, xn=xn, pT=pT: e.transpose(out=pT[:, kd, :], in_=xn[:, kd * 128:(kd + 1) * 128],
                                                                          identity=idb[:]),
                          reads=[bxn, bidb], writes=[bbank[tb]], inc=(kd == 7))
                fw.op("dve", lambda e, blk=blk, pT=pT: e.tensor_copy(out=uT[:, :, blk * 128:(blk + 1) * 128], in_=pT),
                      reads=[bbank[tb]], writes=[buT])

        with ExitStack() as s1:
            wg, bwg = fw.sbuf(s1, "wg", [128, 8, 2 * D], BF16)
            wrb, bwrb = fw.sbuf(s1, "wrb", [128, 8, D], BF16)
            wdb, bwdb = fw.sbuf(s1, "wdb", [128, 8, D], BF16)
            wo, bwo = fw.sbuf(s1, "wo", [128, 8, D], BF16)
            stg = [fw.sbuf(s1, "stg%d" % i, [128, 1024], F32) for i in range(2)]
            load_cast_weight(fw, stg, wg, bwg, wg_d, 8, 2 * D, scale=gmix, bscale=bgmix)
            load_cast_weight(fw, stg, wrb, bwrb, wrb_d, 8, D)
            load_cast_weight(fw, stg, wdb, bwdb, wdb_d, 8, D)
            load_cast_weight(fw, stg, wo, bwo, wo_d, 8, D)
            xt = [fw.sbuf(s1, "xt%d" % i, [128, D], F32) for i in range(4)]
            xn_bufs = [fw.sbuf(s1, "xn%d" % i, [128, D], BF16) for i in range(2)]
            junk, bjunk = fw.sbuf(s1, "junk", [128, D], BF16)
            uT, buT = fw.sbuf(s1, "uT", [128, 8, 512], BF16)
            ogt, bogt = fw.sbuf(s1, "ogt", [128, 16, 512], BF16)
            sg = [fw.sbuf(s1, "sg%d" % i, [128, 512], F32) for i in range(4)]
            tt = [fw.sbuf(s1, "tt%d" % i, [128, 512], F32) for i in range(4)]
            mT, bmT = fw.sbuf(s1, "mT", [128, 8, 512], BF16)
            T0, T1, GR, GD, YR, YD, H0, H1 = range(8)
            ogT_v = ogT.rearrange("(c p) t -> p c t", p=128)
            blk0 = 0
            ti = 0
            while blk0 < NBLK:
                nblk = 1 if blk0 == 0 else min(4, NBLK - blk0)
                W = nblk * 128
                t0 = blk0 * 128
                for blk in range(nblk):
                    fw.dma("sp", xt[blk][0][:], xB[t0 + blk * 128:t0 + (blk + 1) * 128, :], writes=[xt[blk][1]])
                fw.dma("sp", ogt[:, :, 0:W], ogT_v[:, :, t0:t0 + W], writes=[bogt])
                norm_transpose([x[0] for x in xt], [x[1] for x in xt], nblk, xn_bufs, uT, buT, (T0, T1), junk, bjunk)
                for oc in range(8):
                    cs = slice(oc * 128, (oc + 1) * 128)
                    cs2 = slice(D + oc * 128, D + (oc + 1) * 128)
                    for kd in range(8):
                        fw.op("pe", lambda e, kd=kd: e.matmul(banks[GR][:, 0:W], lhsT=wg[:, kd, cs], rhs=uT[:, kd, 0:W],
                                                              start=(kd == 0), stop=(kd == 7)),
                              reads=[bwg, buT], writes=[bbank[GR]], inc=(kd == 7))
                    for kd in range(8):
                        fw.op("pe", lambda e, kd=kd: e.matmul(banks[GD][:, 0:W], lhsT=wg[:, kd, cs2], rhs=uT[:, kd, 0:W],
                                                              start=(kd == 0), stop=(kd == 7)),
                              reads=[bwg, buT], writes=[bbank[GD]], inc=(kd == 7))
                    for c in range(8):
                        fw.op("pe", lambda e, c=c: e.matmul(banks[YR][:, 0:W], lhsT=wrb[:, c, cs], rhs=ogt[:, c, 0:W],
                                                            start=(c == 0), stop=(c == 7)),
                              reads=[bwrb, bogt], writes=[bbank[YR]], inc=(c == 7))
                    for c in range(8):
                        fw.op("pe", lambda e, c=c: e.matmul(banks[YD][:, 0:W], lhsT=wdb[:, c, cs], rhs=ogt[:, 8 + c, 0:W],
                                                            start=(c == 0), stop=(c == 7)),
                              reads=[bwdb, bogt], writes=[bbank[YD]], inc=(c == 7))
                    p = (oc % 2) * 2
                    sr, bsr = sg[p]
                    sd, bsd = sg[p + 1]
                    t1, bt1 = tt[p]
                    t2, bt2 = tt[p + 1]
                    fw.op("act", lambda e: e.activation(out=sr[:, 0:W], in_=banks[GR][:, 0:W], func=AF.Sigmoid),
                          reads=[bbank[GR]], writes=[bsr])
                    fw.op("act", lambda e: e.activation(out=sd[:, 0:W], in_=banks[GD][:, 0:W], func=AF.Sigmoid),
                          reads=[bbank[GD]], writes=[bsd])
                    fw.op("dve", lambda e: e.tensor_tensor(out=t1[:, 0:W], in0=banks[YR][:, 0:W], in1=sr[:, 0:W], op=ALU.mult),
                          reads=[bbank[YR], bsr], writes=[bt1])
                    fw.op("dve", lambda e: e.tensor_tensor(out=t2[:, 0:W], in0=banks[YD][:, 0:W], in1=sd[:, 0:W], op=ALU.mult),
                          reads=[bbank[YD], bsd], writes=[bt2])
                    fw.op("pool", lambda e: e.tensor_tensor(out=mT[:, oc, 0:W], in0=t1[:, 0:W], in1=t2[:, 0:W], op=ALU.add),
                          reads=[bt1, bt2], writes=[bmT])
                n = 0
                for blk in range(nblk):
                    for dh in range(2):
                        hb = (H0, H1)[n % 2]
                        n += 1
                        for mc in range(8):
                            fw.op("pe", lambda e, mc=mc: e.matmul(banks[hb][:, :], lhsT=mT[:, mc, blk * 128:(blk + 1) * 128],
                                                                  rhs=wo[:, mc, dh * 512:(dh + 1) * 512],
                                                                  start=(mc == 0), stop=(mc == 7)),
                                  reads=[bmT, bwo], writes=[bbank[hb]], inc=(mc == 7))
                        xs = xt[blk][0][:, dh * 512:(dh + 1) * 512]
                        fw.op("dve", lambda e, xs=xs, hb=hb: e.tensor_tensor(out=xs, in0=banks[hb][:, :], in1=xs, op=ALU.add),
                              reads=[bbank[hb], xt[blk][1]], writes=[xt[blk][1]])
                    fw.dma("sp", hS[t0 + blk * 128:t0 + (blk + 1) * 128, :], xt[blk][0][:], reads=[xt[blk][1]], writes=[bhS])
                blk0 += nblk
                ti += 1

        fw.barrier()
        with ExitStack() as s2:
            wup, bwup = fw.sbuf(s2, "wup", [128, 8, 2 * DFF], BF16)
            wdn, bwdn = fw.sbuf(s2, "wdn", [128, 22, D], BF16)
            stg = [fw.sbuf(s2, "stgb%d" % i, [128, 1024], F32) for i in range(2)]
            load_cast_weight(fw, stg, wup, bwup, wup_d, 8, 2 * DFF, scale=gffn, bscale=bgffn)
            load_cast_weight(fw, stg, wdn, bwdn, wdn_d, 22, D)
            WT = 256
            ht = [fw.sbuf(s2, "ht%d" % i, [128, D], F32) for i in range(2)]
            xn_bufs = [fw.sbuf(s2, "hn%d" % i, [128, D], BF16) for i in range(2)]
            junk, bjunk = fw.sbuf(s2, "junk2", [128, D], BF16)
            u2T, bu2T = fw.sbuf(s2, "u2T", [128, 8, WT], BF16)
            gT, bgT = fw.sbuf(s2, "gT", [128, 22, WT], BF16)
            hist, bhist = fw.sbuf(s2, "hist", [128, NFC, 2], F32)
            raw = [fw.sbuf(s2, "raw%d" % i, [128, 2 + WT], F32) for i in range(4)]
            acc = [fw.sbuf(s2, "acc%d" % i, [128, WT], F32) for i in range(4)]
            osb = [fw.sbuf(s2, "osb%d" % i, [128, D], F32) for i in range(2)]
            fw.op("pool", lambda e: e.memset(hist[:], 0.0), writes=[bhist])
            T0, T1, UA0, UB0, UA1, UB1, O0, O1 = range(8)
            blk0 = 0
            nrot = 0
            nout = 0
            while blk0 < NBLK:
                nblk = 1 if blk0 == 0 else 2
                W = nblk * 128
                t0 = blk0 * 128
                for blk in range(nblk):
                    fw.dma("sp", ht[blk][0][:], hS[t0 + blk * 128:t0 + (blk + 1) * 128, :], reads=[bhS], writes=[ht[blk][1]])
                norm_transpose([x[0] for x in ht], [x[1] for x in ht], nblk, xn_bufs, u2T, bu2T, (T0, T1), junk, bjunk)
                for j in range(22):
                    par = j % 2
                    accs = []
                    for half, c in enumerate((j, 22 + j)):
                        pb = ((UA0, UB0), (UA1, UB1))[par][half]
                        for kd in range(8):
                            fw.op("pe", lambda e, kd=kd, c=c, pb=pb: e.matmul(banks[pb][:, 0:W], lhsT=wup[:, kd, c * 128:(c + 1) * 128],
                                                                            rhs=u2T[:, kd, 0:W], start=(kd == 0), stop=(kd == 7)),
                                  reads=[bwup, bu2T], writes=[bbank[pb]], inc=(kd == 7))
                        rw, brw = raw[nrot % 4]
                        ac, bac = acc[nrot % 4]
                        nrot += 1
                        fw.op("pool", lambda e, rw=rw, c=c: e.tensor_copy(out=rw[:, 0:2], in_=hist[:, c, :]),
                              reads=[bhist], writes=[brw])
                        fw.op("act", lambda e, rw=rw, pb=pb: e.activation(out=rw[:, 2:2 + W], in_=banks[pb][:, 0:W], func=AF.Copy),
                              reads=[bbank[pb]], writes=[brw])
                        fw.op("pool", lambda e, rw=rw, c=c: e.tensor_copy(out=hist[:, c, :], in_=rw[:, W:W + 2]),
                              reads=[brw], writes=[bhist])
                        fw.op("act", lambda e, ac=ac, pb=pb, c=c: e.activation(out=ac[:, 0:W], in_=banks[pb][:, 0:W], func=AF.Identity,
                                                                              scale=fcw[:, c, 2:3], bias=fcb[:, c:c + 1]),
                              reads=[bbank[pb], bfcw, bfcb], writes=[bac])
                        fw.op("dve", lambda e, ac=ac, rw=rw, c=c: e.scalar_tensor_tensor(
                            out=ac[:, 0:W], in0=rw[:, 1:1 + W], scalar=fcw[:, c, 1:2], in1=ac[:, 0:W], op0=ALU.mult, op1=ALU.add),
                            reads=[brw, bac, bfcw], writes=[bac])
                        fw.op("dve", lambda e, ac=ac, rw=rw, c=c: e.scalar_tensor_tensor(
                            out=ac[:, 0:W], in0=rw[:, 0:W], scalar=fcw[:, c, 0:1], in1=ac[:, 0:W], op0=ALU.mult, op1=ALU.add),
                            reads=[brw, bac, bfcw], writes=[bac])
                        accs.append((ac, bac))
                    if blk0 > 0:
                        (aa, baa), (ab, bab) = accs
                        fw.op("act", lambda e, aa=aa: e.activation(out=aa[:, 0:W], in_=aa[:, 0:W], func=AF.Silu),
                              reads=[baa], writes=[baa])
                        fw.op("pool", lambda e, aa=aa, ab=ab, j=j: e.tensor_tensor(out=gT[:, j, 0:W], in0=aa[:, 0:W], in1=ab[:, 0:W], op=ALU.mult),
                              reads=[baa, bab], writes=[bgT])
                if blk0 > 0:
                    for blk in range(nblk):
                        for dh in range(2):
                            ob = (O0, O1)[dh]
                            for j in range(22):
                                fw.op("pe", lambda e, j=j, ob=ob: e.matmul(banks[ob][:, :], lhsT=gT[:, j, blk * 128:(blk + 1) * 128],
                                                                         rhs=wdn[:, j, dh * 512:(dh + 1) * 512],
                                                                         start=(j == 0), stop=(j == 21)),
                                      reads=[bgT, bwdn], writes=[bbank[ob]], inc=(j == 21))
                            hs = ht[blk][0][:, dh * 512:(dh + 1) * 512]
                            fw.op("dve", lambda e, hs=hs, ob=ob: e.tensor_tensor(out=hs, in0=banks[ob][:, :], in1=hs, op=ALU.add),
                                  reads=[bbank[ob], ht[blk][1]], writes=[ht[blk][1]])
                        fw.op("act", lambda e, blk=blk: e.activation(out=junk[:], in_=ht[blk][0][:], func=AF.Square,
                                                                     accum_out=ss[:, 4 + blk:5 + blk]),
                              reads=[ht[blk][1]], writes=[bjunk, bss])
                        fw.op("dve", lambda e, blk=blk: e.tensor_scalar(out=tmpS[:, 4 + blk:5 + blk], in0=ss[:, 4 + blk:5 + blk],
                                                                        scalar1=1.0 / D, scalar2=EPS, op0=ALU.mult, op1=ALU.add),
                              reads=[bss], writes=[btmpS])
                        fw.op("act", lambda e, blk=blk: e.activation(out=tmpS[:, 4 + blk:5 + blk], in_=tmpS[:, 4 + blk:5 + blk], func=AF.Sqrt),
                              reads=[btmpS], writes=[btmpS])
                        fw.op("dve", lambda e, blk=blk: e.reciprocal(out=rstd[:, 4 + blk:5 + blk], in_=tmpS[:, 4 + blk:5 + blk]),
                              reads=[btmpS], writes=[brstd])
                        ot, bot = osb[nout % 2]
                        nout += 1
                        fw.op("act", lambda e, blk=blk, ot=ot: e.activation(out=ot[:], in_=ht[blk][0][:], func=AF.Copy,
                                                                           scale=rstd[:, 4 + blk:5 + blk]),
                              reads=[ht[blk][1], brstd], writes=[bot])
                        fw.op("pool", lambda e, ot=ot: e.tensor_tensor(out=ot[:], in0=ot[:], in1=gfin[:], op=ALU.mult),
                              reads=[bot, bgfin], writes=[bot])
                        r0 = t0 - 128 + blk * 128
                        fw.dma("sp", out_d[r0:r0 + 128, :], ot[:], reads=[bot], writes=[bout])
                blk0 += nblk
            fw.finish([bout])
    return nc


def stageB_inputs(inp, b, r, TB, og_full_T):
    l = 0
    T = inp["x"].shape[1]
    t_start = r * TB
    NBLK = TB // 128 + 1
    xB = np.zeros((NBLK * 128, D), np.float32)
    ogT = np.zeros((2 * D, NBLK * 128), ml_dtypes.bfloat16)
    lo = t_start - 128
    if lo >= 0:
        xB[:] = inp["x"][b, lo:t_start + TB]
        ogT[:] = og_full_T[:, lo:t_start + TB]
    else:
        xB[128:] = inp["x"][b, t_start:t_start + TB]
        ogT[:, 128:] = og_full_T[:, t_start:t_start + TB]
    w_in = inp["w_in"][l]
    c0 = 9232 - 2048
    perpart = lambda v: np.ascontiguousarray(v.reshape(-1, 128).T)
    m = {
        "xB": xB, "ogT": ogT,
        "wg": np.ascontiguousarray(w_in[:, c0:]),
        "wrb": inp["w_ret_br"][l], "wdb": inp["w_dn_br"][l], "wo": inp["w_o"][l],
        "wup": inp["w_up"][l], "wdn": inp["w_down"][l],
        "gmix": perpart(inp["g_mix"][l]), "gffn": perpart(inp["g_ffn"][l]),
        "fcw": np.ascontiguousarray(inp["ffn_conv_w"][l].reshape(3, NFC, 128).transpose(2, 1, 0)),
        "fcb": perpart(inp["ffn_conv_b"][l]),
        "gfin": np.ascontiguousarray(np.broadcast_to(inp["g_final"][None, :], (128, D))),
        "ident": np.eye(128, dtype=np.float32),
    }
    return {k: np.ascontiguousarray(v) for k, v in m.items()}


def build_A(T):
    NCH = T // 128
    nc = bass.Bass("TRN2", target_bir_lowering=False)
    dt = lambda name, shape, dty=F32, kind="ExternalInput": nc.dram_tensor(name, list(shape), dty, kind=kind).ap()
    xA = dt("xA", [T, D])
    wtm_d = dt("wtm", [D, 2056])
    wfm_d = dt("wfm", [D, 1536])
    gmix_d = dt("gmix", [128, 8])
    cos_d = dt("cosT", [128, NCH, 32])
    sin_d = dt("sinT", [128, NCH, 32])
    sct_d = dt("sctab", [128, 8])
    cst_d = dt("cst", [128, 8, 512])
    sq_d = dt("sqm", [128, 3, 128])
    cw_d = dt("dcw", [128, 12, 4])
    ab_d = dt("abt", [128, 8])
    ogA = dt("ogA", [8 * 128, T], BF16, "ExternalOutput")
    bog = Buf("ogA")

    with ExitStack() as st:
        fw = FW(nc, st)
        V = lambda en, fn, r=(), w=(), inc=True: fw.op(en, fn, reads=r, writes=w, inc=inc)
        SB = lambda name, shape, dty=F32: fw.sbuf(st, name, shape, dty)
        gmix, bgmix = SB("gmix", [128, 8])
        cosT, bcos = SB("cosT", [128, NCH, 32])
        sinT, bsin = SB("sinT", [128, NCH, 32])
        sct, bsct = SB("sct", [128, 8])
        cst, bcst = SB("cst", [128, 8, 512])
        sqm, bsqm = SB("sqm", [128, 3, 128])
        dcw, bdcw = SB("dcw", [128, 12, 4])
        abt, babt = SB("abt", [128, 8])
        for s_, d_, b_ in ((gmix, gmix_d, bgmix), (cosT, cos_d, bcos), (sinT, sin_d, bsin), (sct, sct_d, bsct),
                           (cst, cst_d, bcst), (sqm, sq_d, bsqm), (dcw, cw_d, bdcw), (abt, ab_d, babt)):
            fw.dma("sp", s_[:], d_, writes=[b_])
        v4 = lambda ap: ap.rearrange("p (h c) -> p h c", h=4)
        maskU4, negU4, posL4, ones4 = (v4(cst[:, i, :]) for i in range(4))
        retg, dng, cdt = cst[:, 5, :], cst[:, 6, :], cst[:, 7, :]
        triU, triLs, onesf = sqm[:, 0, :], sqm[:, 1, :], sqm[:, 2, :]
        idb, bidb = SB("idb", [128, 128], BF16)
        id4, bid4 = SB("id4", [128, 4, 128], BF16)
        onesb, bonesb = SB("onesb", [128, 128], BF16)
        negA, bnegA = SB("negA", [128, 4])
        V("dve", lambda e: e.tensor_copy(out=id4[:], in_=v4(cst[:, 4, :])), [bcst], [bid4])
        V("dve", lambda e: e.tensor_copy(out=idb[:], in_=cst[:, 4, 0:128]), [bcst], [bidb])
        V("dve", lambda e: e.tensor_copy(out=onesb[:], in_=onesf), [bsqm], [bonesb])
        triUb, btrib = SB("triUb", [128, 128], BF16)
        triLsb, _ = SB("triLsb", [128, 128], BF16)
        V("dve", lambda e: e.tensor_copy(out=triUb[:], in_=triU), [bsqm], [btrib])
        V("dve", lambda e: e.tensor_copy(out=triLsb[:], in_=triLs), [bsqm], [btrib])
        V("act", lambda e: e.activation(out=negA[:], in_=abt[:, 4:8], func=AF.Exp), [babt], [bnegA])
        V("dve", lambda e: e.tensor_scalar(out=negA[:], in0=negA[:], scalar1=-1.0, scalar2=None, op0=ALU.mult), [bnegA], [bnegA])
        wtm, bwtm = SB("wtm", [128, 8, 2056], BF16)
        wfm, bwfm = SB("wfm", [128, 8, 1536], BF16)
        stg = [SB("stg%d" % i, [128, 1024]) for i in range(2)]
        load_cast_weight(fw, stg, wtm, bwtm, wtm_d, 8, 2056, scale=gmix, bscale=bgmix)
        load_cast_weight(fw, stg, wfm, bwfm, wfm_d, 8, 1536, scale=gmix, bscale=bgmix)
        banks = [fw.psum(st, "bank%d" % i, [128, 512], F32) for i in range(8)]
        bbank = [Buf("bank%d" % i) for i in range(8)]
        for b_ in bbank:
            b_.psum = True
        rr = [0]

        def nb():
            i = rr[0] % 8
            rr[0] += 1
            return banks[i], bbank[i]
        bfv = lambda bank: bank[:].bitcast(BF16).rearrange("p (g c) -> p g c", g=8)
        bc4 = lambda ap: ap.rearrange("p (h o) -> p h o", o=1).to_broadcast([128, 4, 128])

        xt = [SB("xt%d" % i, [128, D]) for i in range(2)]
        junkb, bjunkb = SB("junkb", [128, D], BF16)
        xn, bxn = SB("xn", [128, D], BF16)
        uT, buT = SB("uT", [128, 8, 128], BF16)
        ss, bss = SB("ss", [128, 16])
        tmpS, btmpS = SB("tmpS", [128, 16])
        rstd, brstd = SB("rstd", [128, 16])
        qk_sb, bqk = SB("qk_sb", [128, 512])
        rt = [SB("rt%d" % i, [128, 8, 32]) for i in range(4)]
        qr, bqr = SB("qr", [128, 8, 2, 32])
        qrs, bqrs = SB("qrs", [128, 8, 64], BF16)
        qkT, bqkT = SB("qkT", [64, 8, 128], BF16)
        sc_bf, bsc = SB("sc_bf", [128, 4, 128], BF16)
        v_bf, bvbf = SB("v_bf", [128, 4, 128], BF16)
        Sr, bSr = SB("Sr", [64, 512])
        Sr_bf, bSrbf = SB("Sr_bf", [64, 4, 128], BF16)
        o_sb, bosb = SB("o_sb", [128, 4, 128])
        junkf, bjunkf = SB("junkf", [128, 4, 128])
        sg, bsg = SB("sg", [128, 512])
        sz, bsz = SB("sz", [128, 512])
        og, bogs = SB("og", [128, 512], BF16)
        ogst = [SB("ogst%d" % i, [128, 8, 128], BF16) for i in range(2)]
        raw, braw = SB("raw", [128, 12, 131])
        acc, bacc = SB("acc", [128, 12, 128])
        sqb, bsqb = SB("sqb", [128, 8, 128], BF16)
        tq, btq = SB("tq", [128, 8, 128])
        qkn, bqkn = SB("qkn", [128, 8, 128], BF16)
        vbT, bvbT = SB("vbT", [128, 4, 128], BF16)
        ktv, bktv = SB("ktv", [128, 8, 128], BF16)
        bd, bbd = SB("bd", [128, 16])
        Et, bEt = SB("Et", [128, 16])
        G_sb, bG = SB("G_sb", [128, 4])
        gbc, bgbc = SB("gbc", [128, 3, 4, 128], BF16)
        gs, bgs = SB("gs", [128, 12], BF16)
        gr_, bgr = SB("gr_", [128, 8])
        X, bX = SB("X", [128, 4, 128])
        XU, bXU = SB("XU", [128, 4, 128])
        XL, bXL = SB("XL", [128, 4, 128])
        eGr, beGr = SB("eGr", [128, 4, 128])
        tmpM, btmpM = SB("tmpM", [128, 4, 128])
        Mb = [SB("Mb%d" % i, [128, 4, 128], BF16) for i in range(2)]
        Nb = [SB("Nb%d" % i, [128, 4, 128], BF16) for i in range(2)]
        Qb = [SB("Qb%d" % i, [128, 4, 128], BF16) for i in range(2)]
        attnT, battn = SB("attnT", [128, 4, 128], BF16)
        kbg, bkbg = SB("kbg", [128, 4, 128], BF16)
        kdec, bkdec = SB("kdec", [128, 4, 128], BF16)
        vb, bvb = SB("vb", [128, 4, 128], BF16)
        wT_bf, bwT = SB("wT_bf", [128, 4, 128], BF16)
        u_sb, busb = SB("u_sb", [128, 4, 128])
        qdT, bqdT = SB("qdT", [128, 4, 128], BF16)
        vn, bvn = SB("vn", [128, 4, 128], BF16)
        Sd, bSd = SB("Sd", [128, 4, 128])
        Sd_bf, bSdbf = SB("Sd_bf", [128, 4, 128], BF16)
        do_sb, bdo = SB("do_sb", [128, 4, 128])
        V("pool", lambda e: e.memset(Sr[:], 0.0), (), [bSr])
        V("pool", lambda e: e.memset(Sr_bf[:], 0.0), (), [bSrbf])
        V("pool", lambda e: e.memset(Sd[:], 0.0), (), [bSd])
        V("pool", lambda e: e.memset(Sd_bf[:], 0.0), (), [bSdbf])
        V("pool", lambda e: e.memset(raw[:], 0.0), (), [braw])

        ogA_v = ogA.rearrange("(c p) t -> p c t", p=128)

        def mm4(bank, bbk, lhs_fn, rhs_fn, reads, nparts=128):
            bv = v4(bank[:])
            for h in range(4):
                V("pe", lambda e, h=h: e.matmul(bv[0:nparts, h, :], lhsT=lhs_fn(h), rhs=rhs_fn(h), start=True, stop=True),
                  reads, [bbk], inc=(h == 3))

        def tr4(src_fn, reads, dst, bdst, evac_en="act", nin=4, g0=0):
            bank, bbk = nb()
            pv = bfv(bank)
            for h in range(nin):
                V("pe", lambda e, h=h: e.transpose(out=pv[:, h, :], in_=src_fn(h), identity=idb[:]),
                  list(reads) + [bidb], [bbk], inc=(h == nin - 1))
            if evac_en == "act":
                V("act", lambda e: e.activation(out=dst[:, g0:g0 + nin, :], in_=pv[:, 0:nin, :], func=AF.Copy), [bbk], [bdst])
            else:
                V(evac_en, lambda e: e.tensor_copy(out=dst[:, g0:g0 + nin, :], in_=pv[:, 0:nin, :]), [bbk], [bdst])

        def mark(nm):
            if n == 0:
                pass
        for n in range(NCH):
            x_t, bx = xt[n % 2]
            ost, bost = ogst[n % 2]
            mark('# ---------------- A1: load + rmsnorm + ')
            fw.dma("sp", x_t[:], xA[n * 128:(n + 1) * 128, :], writes=[bx])
            V("act", lambda e: e.activation(out=junkb[:], in_=x_t[:], func=AF.Square, accum_out=ss[:, 0:1]), [bx], [bjunkb, bss])
            rms_stats(fw, ss, bss, tmpS, btmpS, rstd, brstd, 1, 1.0 / D, EPS)
            V("act", lambda e: e.activation(out=xn[:], in_=x_t[:], func=AF.Copy, scale=rstd[:, 0:1]), [bx, brstd], [bxn])
            bank, bbk = nb()
            pT = bfv(bank)
            for kd in range(8):
                V("pe", lambda e, kd=kd: e.transpose(out=pT[:, kd, :], in_=xn[:, kd * 128:(kd + 1) * 128], identity=idb[:]),
                  [bxn, bidb], [bbk], inc=(kd == 7))
            V("dve", lambda e: e.tensor_copy(out=uT[:], in_=pT), [bbk], [buT])
            mark('# ---------------- A2: token-major proje')
            pg = []
            for g in range(5):
                bank, bbk = nb()
                c0, cw_ = (g * 512, 512) if g < 4 else (2048, 8)
                for kd in range(8):
                    V("pe", lambda e, kd=kd, bank=bank: e.matmul(bank[:, 0:cw_], lhsT=uT[:, kd, :], rhs=wtm[:, kd, c0:c0 + cw_],
                                                               start=(kd == 0), stop=(kd == 7)),
                      [buT, bwtm], [bbk], inc=(kd == 7))
                pg.append((bank, bbk))
            mark('# early evacuations of token-major group')
            V("act", lambda e: e.activation(out=qk_sb[:], in_=pg[0][0][:], func=AF.Copy), [pg[0][1]], [bqk])
            V("act", lambda e: e.activation(out=v_bf[:], in_=v4(pg[1][0][:]), func=AF.Copy), [pg[1][1]], [bvbf])
            V("act", lambda e: e.activation(out=sg[:], in_=pg[2][0][:], func=AF.Silu), [pg[2][1]], [bsg])
            V("act", lambda e: e.activation(out=sz[:], in_=pg[3][0][:], func=AF.Silu), [pg[3][1]], [bsz])
            V("act", lambda e: e.activation(out=bd[:, 0:4], in_=pg[4][0][:, 0:4], func=AF.Sigmoid), [pg[4][1]], [bbd])
            V("dve", lambda e: e.tensor_tensor(out=bd[:, 4:8], in0=pg[4][0][:, 4:8], in1=abt[:, 0:4], op=ALU.add), [pg[4][1], babt], [bbd])
            V("act", lambda e: e.activation(out=bd[:, 4:8], in_=bd[:, 4:8], func=AF.Exp), [bbd], [bbd])
            V("act", lambda e: e.activation(out=bd[:, 4:8], in_=bd[:, 4:8], func=AF.Ln, bias=1.0), [bbd], [bbd])
            V("dve", lambda e: e.tensor_tensor(out=bd[:, 8:12], in0=bd[:, 4:8], in1=negA[:], op=ALU.mult), [bbd, bnegA], [bbd])
            V("dve", lambda e: e.tensor_scalar(out=bd[:, 12:16], in0=bd[:, 0:4], scalar1=-1.0, scalar2=None, op0=ALU.mult), [bbd], [bbd])
            mark('# ---------------- A3: feature-major pro')
            pf = []
            for qkv in range(3):
                bank, bbk = nb()
                bv = v4(bank[:])
                for hl in range(4):
                    c = qkv * 4 + hl
                    for kd in range(8):
                        V("pe", lambda e, kd=kd, c=c, hl=hl, bv=bv: e.matmul(bv[:, hl, :], lhsT=wfm[:, kd, c * 128:(c + 1) * 128], rhs=uT[:, kd, :],
                                                                          start=(kd == 0), stop=(kd == 7)),
                          [buT, bwfm], [bbk], inc=(kd == 7 and hl == 3))
                pf.append((bank, bbk))
            mark('# ---------------- retention')
            x1 = qk_sb[:].rearrange("p (g t f) -> p g t f", g=8, t=2)[:, :, 0, :]
            x2 = qk_sb[:].rearrange("p (g t f) -> p g t f", g=8, t=2)[:, :, 1, :]
            cb = cosT[:, n, :].rearrange("p (o f) -> p o f", o=1).to_broadcast([128, 8, 32])
            sb_ = sinT[:, n, :].rearrange("p (o f) -> p o f", o=1).to_broadcast([128, 8, 32])
            (r0, b0), (r1, b1), (r2, b2), (r3, b3) = rt
            V("dve", lambda e: e.tensor_tensor(out=r0[:], in0=x1, in1=cb, op=ALU.mult), [bqk, bcos], [b0])
            V("dve", lambda e: e.tensor_tensor(out=r1[:], in0=x2, in1=sb_, op=ALU.mult), [bqk, bsin], [b1])
            V("pool", lambda e: e.tensor_tensor(out=r2[:], in0=x1, in1=sb_, op=ALU.mult), [bqk, bsin], [b2])
            V("pool", lambda e: e.tensor_tensor(out=r3[:], in0=x2, in1=cb, op=ALU.mult), [bqk, bcos], [b3])
            V("dve", lambda e: e.tensor_tensor(out=qr[:, :, 0, :], in0=r0[:], in1=r1[:], op=ALU.subtract), [b0, b1], [bqr])
            V("pool", lambda e: e.tensor_tensor(out=qr[:, :, 1, :], in0=r2[:], in1=r3[:], op=ALU.add), [b2, b3], [bqr])
            V("dve", lambda e: e.tensor_tensor(out=qrs[:], in0=qr[:].rearrange("p g t f -> p g (t f)"),
                                               in1=sct[:].rearrange("p (g o) -> p g o", o=1).to_broadcast([128, 8, 64]), op=ALU.mult),
              [bqr, bsct], [bqrs])
            bank, bbk = nb()
            pv = bfv(bank)
            for g in range(8):
                V("pe", lambda e, g=g: e.transpose(out=pv[0:64, g, :], in_=qrs[:, g, :], identity=idb[:]), [bqrs, bidb], [bbk], inc=(g == 7))
            V("act", lambda e: e.activation(out=qkT[:], in_=pv[0:64, :, :], func=AF.Copy), [bbk], [bqkT])
            bank, bbk = nb()
            mm4(bank, bbk, lambda h: qkT[:, 4 + h, :], lambda h: qkT[:, h, :], [bqkT])
            V("dve", lambda e, bank=bank: e.tensor_tensor(out=sc_bf[:], in0=v4(bank[:]), in1=maskU4, op=ALU.mult), [bbk, bcst], [bsc])
            bank_o, bbk_o = nb()
            bvo = v4(bank_o[:])
            for h in range(4):
                V("pe", lambda e, h=h: e.matmul(bvo[:, h, :], lhsT=sc_bf[:, h, :], rhs=v_bf[:, h, :], start=True, stop=False),
                  [bsc, bvbf], [bbk_o], inc=False)
                V("pe", lambda e, h=h: e.matmul(bvo[:, h, :], lhsT=qkT[:, h, :], rhs=Sr_bf[:, h, :], start=False, stop=True),
                  [bqkT, bSrbf], [bbk_o], inc=(h == 3))
            bank_kv, bbk_kv = nb()
            mm4(bank_kv, bbk_kv, lambda h: qrs[:, 4 + h, :], lambda h: v_bf[:, h, :], [bqrs, bvbf], nparts=64)
            V("dve", lambda e: e.tensor_tensor(out=Sr[:], in0=bank_kv[0:64, :], in1=Sr[:], op=ALU.add), [bbk_kv, bSr], [bSr])
            V("pool", lambda e: e.tensor_tensor(out=Sr[:], in0=Sr[:], in1=cdt[0:64, :], op=ALU.mult), [bSr, bcst], [bSr])
            V("pool", lambda e: e.tensor_copy(out=Sr_bf[:], in_=v4(Sr[:])), [bSr], [bSrbf])
            mark('# group norm + gate')
            V("act", lambda e: e.activation(out=o_sb[:], in_=bvo, func=AF.Copy), [bbk_o], [bosb])
            V("dve", lambda e: e.tensor_reduce(out=ss[:, 4:8], in_=o_sb[:], axis=AX.X, op=ALU.add), [bosb], [bss])
            V("act", lambda e: e.activation(out=junkf[:], in_=o_sb[:], func=AF.Square), [bosb], [bjunkf])
            V("dve", lambda e: e.tensor_reduce(out=ss[:, 8:12], in_=junkf[:], axis=AX.X, op=ALU.add), [bjunkf], [bss])
            V("dve", lambda e: e.tensor_scalar(out=tmpS[:, 4:8], in0=ss[:, 4:8], scalar1=1.0 / 128, scalar2=None, op0=ALU.mult), [bss], [btmpS])
            V("dve", lambda e: e.tensor_tensor(out=tmpS[:, 8:12], in0=tmpS[:, 4:8], in1=tmpS[:, 4:8], op=ALU.mult), [btmpS], [btmpS])
            V("dve", lambda e: e.scalar_tensor_tensor(out=tmpS[:, 12:16], in0=ss[:, 8:12], scalar=1.0 / 128, in1=tmpS[:, 8:12],
                                                      op0=ALU.mult, op1=ALU.subtract), [bss, btmpS], [btmpS])
            V("dve", lambda e: e.tensor_scalar(out=tmpS[:, 12:16], in0=tmpS[:, 12:16], scalar1=GN_EPS, scalar2=None, op0=ALU.add), [btmpS], [btmpS])
            V("act", lambda e: e.activation(out=tmpS[:, 12:16], in_=tmpS[:, 12:16], func=AF.Sqrt), [btmpS], [btmpS])
            V("dve", lambda e: e.reciprocal(out=rstd[:, 4:8], in_=tmpS[:, 12:16]), [btmpS], [brstd])
            V("dve", lambda e: e.tensor_tensor(out=o_sb[:], in0=o_sb[:], in1=bc4(tmpS[:, 4:8]), op=ALU.subtract), [bosb, btmpS], [bosb])
            V("dve", lambda e: e.tensor_tensor(out=o_sb[:], in0=o_sb[:], in1=bc4(rstd[:, 4:8]), op=ALU.mult), [bosb, brstd], [bosb])
            V("pool", lambda e: e.tensor_tensor(out=sg[:], in0=sg[:], in1=retg, op=ALU.mult), [bsg, bcst], [bsg])
            V("pool", lambda e: e.tensor_tensor(out=og[:], in0=o_sb[:].rearrange("p h c -> p (h c)"), in1=sg[:], op=ALU.mult), [bosb, bsg], [bogs])
            tr4(lambda h: og[:, h * 128:(h + 1) * 128], [bogs], ost, bost, "act", 4, 0)
            mark('# ---------------- DeltaNet: conv + silu')
            for qkv in range(3):
                bank, bbk = pf[qkv]
                V("act", lambda e, qkv=qkv, bank=bank: e.activation(out=raw[:, qkv * 4:(qkv + 1) * 4, 3:131], in_=v4(bank[:]), func=AF.Copy),
                  [bbk], [braw])
                for hl in range(4):
                    c = qkv * 4 + hl
                    V("act", lambda e, c=c, hl=hl, bank=bank: e.activation(out=acc[:, c, :], in_=v4(bank[:])[:, hl, :], func=AF.Copy,
                                                                       scale=dcw[:, c, 3:4]), [bbk, bdcw], [bacc])
            for c in range(12):
                en = "dve"
                for k in range(3):
                    V(en, lambda e, c=c, k=k: e.scalar_tensor_tensor(out=acc[:, c, :], in0=raw[:, c, k:k + 128], scalar=dcw[:, c, k:k + 1],
                                                                    in1=acc[:, c, :], op0=ALU.mult, op1=ALU.add), [braw, bacc, bdcw], [bacc])
            V("pool", lambda e: e.tensor_copy(out=raw[:, :, 0:3], in_=raw[:, :, 128:131]), [braw], [braw])
            V("act", lambda e: e.activation(out=acc[:], in_=acc[:], func=AF.Silu), [bacc], [bacc])
            mark('# l2 norm of q, k (feature-major): colum')
            V("act", lambda e: e.activation(out=sqb[:], in_=acc[:, 0:8, :], func=AF.Square), [bacc], [bsqb])
            for half in range(2):
                bank, bbk = nb()
                V("pe", lambda e, bank=bank, half=half: e.matmul(bank[:, :], lhsT=onesb[:], rhs=sqb[:, half * 4:(half + 1) * 4, :].rearrange("p h c -> p (h c)"),
                                                                start=True, stop=True), [bonesb, bsqb], [bbk])
                V("dve", lambda e, bank=bank, half=half: e.tensor_scalar(out=tq[:, half * 4:(half + 1) * 4, :], in0=v4(bank[:]), scalar1=EPS, scalar2=None,
                                                                        op0=ALU.add), [bbk], [btq])
            V("act", lambda e: e.activation(out=tq[:], in_=tq[:], func=AF.Sqrt), [btq], [btq])
            V("dve", lambda e: e.reciprocal(out=tq[:], in_=tq[:]), [btq], [btq])
            V("dve", lambda e: e.scalar_tensor_tensor(out=qkn[:, 0:4, :], in0=acc[:, 0:4, :], scalar=128.0 ** -0.5, in1=tq[:, 0:4, :],
                                                      op0=ALU.mult, op1=ALU.mult), [bacc, btq], [bqkn])
            V("pool", lambda e: e.tensor_tensor(out=qkn[:, 4:8, :], in0=acc[:, 4:8, :], in1=tq[:, 4:8, :], op=ALU.mult), [bacc, btq], [bqkn])
            V("act", lambda e: e.activation(out=vbT[:], in_=acc[:, 8:12, :], func=AF.Copy), [bacc], [bvbT])
            mark('# token-major v, k')
            bank, bbk = nb()
            pv = bfv(bank)
            for h in range(4):
                V("pe", lambda e, h=h: e.transpose(out=pv[:, h, :], in_=vbT[:, h, :], identity=idb[:]), [bvbT, bidb], [bbk], inc=False)
            for h in range(4):
                V("pe", lambda e, h=h: e.transpose(out=pv[:, 4 + h, :], in_=qkn[:, 4 + h, :], identity=idb[:]), [bqkn, bidb], [bbk], inc=(h == 3))
            V("act", lambda e: e.activation(out=ktv[:], in_=pv, func=AF.Copy), [bbk], [bktv])
            mark('# g = g1 + g2 + g3 (bf16 pieces)')
            V("dve", lambda e: e.tensor_copy(out=gs[:, 0:4], in_=bd[:, 8:12]), [bbd], [bgs])
            V("dve", lambda e: e.tensor_tensor(out=gr_[:, 0:4], in0=bd[:, 8:12], in1=gs[:, 0:4], op=ALU.subtract), [bbd, bgs], [bgr])
            V("dve", lambda e: e.tensor_copy(out=gs[:, 4:8], in_=gr_[:, 0:4]), [bgr], [bgs])
            V("dve", lambda e: e.tensor_tensor(out=gr_[:, 4:8], in0=gr_[:, 0:4], in1=gs[:, 4:8], op=ALU.subtract), [bgr, bgs], [bgr])
            V("dve", lambda e: e.tensor_copy(out=gs[:, 8:12], in_=gr_[:, 4:8]), [bgr], [bgs])
            bankG, bbkG = nb()
            for ti_, tri_ in enumerate((triUb, triLsb, onesb)):
                for i in range(3):
                    V("pe", lambda e, ti_=ti_, tri_=tri_, i=i: e.matmul(bankG[:, ti_ * 4:(ti_ + 1) * 4], lhsT=tri_[:], rhs=gs[:, i * 4:(i + 1) * 4],
                                                                      start=(i == 0), stop=(i == 2)),
                      [btrib, bonesb, bgs], [bbkG], inc=(ti_ == 2 and i == 2))
            V("act", lambda e: e.activation(out=Et[:, 0:12], in_=bankG[:, 0:12], func=AF.Exp), [bbkG], [bEt])
            V("dve", lambda e: e.tensor_copy(out=G_sb[:], in_=bankG[:, 0:4]), [bbkG], [bG])
            V("dve", lambda e: e.tensor_tensor(out=Et[:, 12:16], in0=Et[:, 0:4], in1=bd[:, 0:4], op=ALU.mult), [bEt, bbd], [bEt])
            for i in range(3):
                V("pool", lambda e, i=i: e.tensor_tensor(out=gbc[:, i, :, :], in0=ones4, in1=bc4(gs[:, i * 4:(i + 1) * 4]), op=ALU.mult),
                  [bcst, bgs], [bgbc])
            bankGb, bbkGb = nb()
            bvg = v4(bankGb[:])
            for h in range(4):
                for i in range(3):
                    V("pe", lambda e, h=h, i=i: e.matmul(bvg[:, h, :], lhsT=gbc[:, i, h, :], rhs=triUb[:], start=(i == 0), stop=(i == 2)),
                      [bgbc, btrib], [bbkGb], inc=(i == 2))
            V("dve", lambda e: e.tensor_tensor(out=X[:], in0=v4(bankGb[:]), in1=bc4(G_sb[:]), op=ALU.subtract), [bbkGb, bG], [bX])
            V("act", lambda e: e.activation(out=eGr[:], in_=v4(bankGb[:]), func=AF.Exp), [bbkGb], [beGr])
            V("dve", lambda e: e.scalar_tensor_tensor(out=XU[:], in0=X[:], scalar=0.0, in1=negU4, op0=ALU.min, op1=ALU.add), [bX, bcst], [bXU])
            V("dve", lambda e: e.scalar_tensor_tensor(out=XL[:], in0=X[:], scalar=0.0, in1=posL4, op0=ALU.max, op1=ALU.add), [bX, bcst], [bXL])
            V("act", lambda e: e.activation(out=XU[:], in_=XU[:], func=AF.Exp), [bXU], [bXU])
            V("act", lambda e: e.activation(out=XL[:], in_=XL[:], func=AF.Exp, scale=-1.0), [bXL], [bXL])
            mark('# KK, QK')
            bankKK, bbkKK = nb()
            mm4(bankKK, bbkKK, lambda h: qkn[:, 4 + h, :], lambda h: qkn[:, 4 + h, :], [bqkn])
            bankQK, bbkQK = nb()
            mm4(bankQK, bbkQK, lambda h: qkn[:, 4 + h, :], lambda h: qkn[:, h, :], [bqkn])
            M0, bM0 = Mb[0]
            N0, bN0 = Nb[0]
            Q0, bQ0 = Qb[0]
            V("dve", lambda e: e.tensor_tensor(out=tmpM[:], in0=v4(bankKK[:]), in1=XL[:], op=ALU.mult), [bbkKK, bXL], [btmpM])
            V("dve", lambda e: e.tensor_tensor(out=M0[:], in0=tmpM[:], in1=bc4(bd[:, 12:16]), op=ALU.mult), [btmpM, bbd], [bM0])
            V("dve", lambda e: e.tensor_tensor(out=attnT[:], in0=v4(bankQK[:]), in1=XU[:], op=ALU.mult), [bbkQK, bXU], [battn])
            tr4(lambda h: M0[:, h, :], [bM0], N0, bN0, "act", 4, 0)
            V("pool", lambda e: e.tensor_tensor(out=Q0[:], in0=N0[:], in1=id4[:], op=ALU.add), [bN0, bid4], [bQ0])
            mark('# ---------------- nilpotent doubling')
            cur = 0
            for k in range(1, 7):
                Mp, bMp = Mb[cur]
                Np, bNp = Nb[cur]
                Qp, bQp = Qb[cur]
                Mn, bMn = Mb[1 - cur]
                Nn, bNn = Nb[1 - cur]
                Qn, bQn = Qb[1 - cur]
                bankM, bbkM = nb()
                mm4(bankM, bbkM, lambda h: Np[:, h, :], lambda h: Mp[:, h, :], [bNp, bMp])
                if k < 6:
                    bankN, bbkN = nb()
                    mm4(bankN, bbkN, lambda h: Mp[:, h, :], lambda h: Np[:, h, :], [bNp, bMp])
                V("act", lambda e: e.activation(out=Mn[:], in_=v4(bankM[:]), func=AF.Copy), [bbkM], [bMn])
                if k < 6:
                    V("dve", lambda e: e.tensor_copy(out=Nn[:], in_=v4(bankN[:])), [bbkN], [bNn])
                bankQ, bbkQ = nb()
                mm4(bankQ, bbkQ, lambda h: Mn[:, h, :], lambda h: Qp[:, h, :], [bMn, bQp])
                V("dve", lambda e: e.tensor_tensor(out=Qn[:], in0=v4(bankQ[:]), in1=Qp[:], op=ALU.add), [bbkQ, bQp], [bQn])
                cur = 1 - cur
            QT, bQT = Qb[cur]
            mark('# ---------------- WY: w^T, u')
            V("dve", lambda e: e.tensor_tensor(out=kbg[:], in0=ktv[:, 4:8, :], in1=bc4(Et[:, 12:16]), op=ALU.mult), [bktv, bEt], [bkbg])
            V("pool", lambda e: e.tensor_tensor(out=kdec[:], in0=ktv[:, 4:8, :], in1=bc4(Et[:, 4:8]), op=ALU.mult), [bktv, bEt], [bkdec])
            V("pool", lambda e: e.tensor_tensor(out=vb[:], in0=ktv[:, 0:4, :], in1=bc4(bd[:, 0:4]), op=ALU.mult), [bktv, bbd], [bvb])
            V("pool", lambda e: e.tensor_tensor(out=qdT[:], in0=qkn[:, 0:4, :], in1=eGr[:], op=ALU.mult), [bqkn, beGr], [bqdT])
            bankW, bbkW = nb()
            mm4(bankW, bbkW, lambda h: kbg[:, h, :], lambda h: QT[:, h, :], [bkbg, bQT])
            bankU, bbkU = nb()
            mm4(bankU, bbkU, lambda h: QT[:, h, :], lambda h: vb[:, h, :], [bvb, bQT])
            V("act", lambda e: e.activation(out=wT_bf[:], in_=v4(bankW[:]), func=AF.Copy), [bbkW], [bwT])
            V("act", lambda e: e.activation(out=u_sb[:], in_=v4(bankU[:]), func=AF.Copy), [bbkU], [busb])
            mark('# ---------------- state-dependent part')
            bankWS, bbkWS = nb()
            mm4(bankWS, bbkWS, lambda h: wT_bf[:, h, :], lambda h: Sd_bf[:, h, :], [bwT, bSdbf])
            V("dve", lambda e: e.tensor_tensor(out=vn[:], in0=u_sb[:], in1=v4(bankWS[:]), op=ALU.subtract), [busb, bbkWS], [bvn])
            bankDO, bbkDO = nb()
            bvd = v4(bankDO[:])
            for h in range(4):
                V("pe", lambda e, h=h: e.matmul(bvd[:, h, :], lhsT=qdT[:, h, :], rhs=Sd_bf[:, h, :], start=True, stop=False),
                  [bqdT, bSdbf], [bbkDO], inc=False)
                V("pe", lambda e, h=h: e.matmul(bvd[:, h, :], lhsT=attnT[:, h, :], rhs=vn[:, h, :], start=False, stop=True),
                  [battn, bvn], [bbkDO], inc=(h == 3))
            bankDS, bbkDS = nb()
            mm4(bankDS, bbkDS, lambda h: kdec[:, h, :], lambda h: vn[:, h, :], [bkdec, bvn])
            for h in range(4):
                V("dve", lambda e, h=h: e.scalar_tensor_tensor(out=Sd[:, h, :], in0=Sd[:, h, :], scalar=Et[:, 8 + h:9 + h],
                                                               in1=v4(bankDS[:])[:, h, :], op0=ALU.mult, op1=ALU.add),
                  [bSd, bEt, bbkDS], [bSd])
            V("pool", lambda e: e.tensor_copy(out=Sd_bf[:], in_=Sd[:]), [bSd], [bSdbf])
            mark('# output rmsnorm + gate')
            V("act", lambda e: e.activation(out=do_sb[:], in_=bvd, func=AF.Copy), [bbkDO], [bdo])
            V("act", lambda e: e.activation(out=junkf[:], in_=do_sb[:], func=AF.Square), [bdo], [bjunkf])
            V("dve", lambda e: e.tensor_reduce(out=ss[:, 12:16], in_=junkf[:], axis=AX.X, op=ALU.add), [bjunkf], [bss])
            V("dve", lambda e: e.tensor_scalar(out=tmpS[:, 0:4], in0=ss[:, 12:16], scalar1=1.0 / 128, scalar2=EPS, op0=ALU.mult, op1=ALU.add),
              [bss], [btmpS])
            V("act", lambda e: e.activation(out=tmpS[:, 0:4], in_=tmpS[:, 0:4], func=AF.Sqrt), [btmpS], [btmpS])
            V("dve", lambda e: e.reciprocal(out=rstd[:, 8:12], in_=tmpS[:, 0:4]), [btmpS], [brstd])
            V("dve", lambda e: e.tensor_tensor(out=do_sb[:], in0=do_sb[:], in1=bc4(rstd[:, 8:12]), op=ALU.mult), [bdo, brstd], [bdo])
            V("pool", lambda e: e.tensor_tensor(out=sz[:], in0=sz[:], in1=dng, op=ALU.mult), [bsz, bcst], [bsz])
            V("pool", lambda e: e.tensor_tensor(out=og[:], in0=do_sb[:].rearrange("p h c -> p (h c)"), in1=sz[:], op=ALU.mult), [bdo, bsz], [bogs])
            tr4(lambda h: og[:, h * 128:(h + 1) * 128], [bogs], ost, bost, "act", 4, 4)
            fw.dma("sp", ogA_v[:, :, n * 128:(n + 1) * 128], ost[:], reads=[bost], writes=[bog])
        fw.finish([bog])
    return nc


def stageA_inputs(inp, b, r, T):
    l = 0
    NCH = T // 128
    w_in = inp["w_in"][l]
    hs = 4 * r
    cols = lambda base, width: np.arange(base + hs * width, base + (hs + 4) * width)
    tm_cols = np.concatenate([cols(0, 64), cols(512, 64), cols(1024, 128), cols(2048, 128), cols(6144, 128),
                              np.arange(7168 + hs, 7168 + hs + 4), np.arange(7176 + hs, 7176 + hs + 4)])
    fm_cols = np.concatenate([cols(3072, 128), cols(4096, 128), cols(5120, 128)])
    perpart = lambda v: np.ascontiguousarray(v.reshape(-1, 128).T)
    p = np.arange(128, dtype=np.float64)
    pos = (np.arange(NCH)[None, :, None] * 128 + p[:, None, None])
    inv = 10000.0 ** (-np.arange(0, 64, 2, dtype=np.float64) / 64)
    ang = (pos.astype(np.float32) * inv.astype(np.float32)[None, None, :]).astype(np.float32)
    cosT = np.cos(ang.astype(np.float64)).astype(np.float32)
    sinT = np.sin(ang.astype(np.float64)).astype(np.float32)
    gam = 1.0 - 2.0 ** (-5.0 - (hs + np.arange(4)))
    sctab = np.zeros((128, 8), np.float64)
    sctab[:, 0:4] = gam[None, :] ** (p[:, None] + 1)
    sctab[:, 4:8] = gam[None, :] ** (-(p[:, None] + 1)) / 8.0
    a = np.arange(128)
    maskU = (a[None, :] >= a[:, None]).astype(np.float32)
    cst = np.zeros((128, 8, 512), np.float32)
    cst[:, 0] = np.tile(maskU, (1, 4))
    cst[:, 1] = np.tile((1 - maskU) * -30000.0, (1, 4))
    cst[:, 2] = np.tile(maskU * 30000.0, (1, 4))
    cst[:, 3] = 1.0
    cst[:, 4] = np.tile(np.eye(128, dtype=np.float32), (1, 4))
    cst[:, 5] = inp["ret_norm_g"][l][hs * 128:(hs + 4) * 128][None, :]
    cst[:, 6] = np.tile(inp["dn_norm_g"][l][None, :], (1, 4))
    cst[:, 7] = np.repeat((gam ** 128).astype(np.float32), 128)[None, :]
    sqm = np.zeros((128, 3, 128), np.float32)
    sqm[:, 0] = maskU
    sqm[:, 1] = 1 - maskU
    sqm[:, 2] = 1.0
    cw = inp["dn_conv_w"][l]
    dcw = np.zeros((128, 12, 4), np.float32)
    for qkv in range(3):
        for hl in range(4):
            ch0 = qkv * 1024 + (hs + hl) * 128
            dcw[:, qkv * 4 + hl, :] = cw[:, ch0:ch0 + 128].T
    abt = np.zeros((128, 8), np.float32)
    abt[:, 0:4] = inp["dn_dt_bias"][l][hs:hs + 4][None, :]
    abt[:, 4:8] = inp["dn_a_log"][l][hs:hs + 4][None, :]
    m = {
        "xA": inp["x"][b, :T], "wtm": w_in[:, tm_cols], "wfm": w_in[:, fm_cols],
        "gmix": perpart(inp["g_mix"][l]), "cosT": cosT, "sinT": sinT, "sctab": sctab.astype(np.float32),
        "cst": cst, "sqm": sqm, "dcw": dcw, "abt": abt,
    }
    return {k: np.ascontiguousarray(v) for k, v in m.items()}


_CACHE = {}


def kernel(x, g_mix, w_in, ret_norm_g, dn_conv_w, dn_a_log, dn_dt_bias, dn_norm_g,
           w_ret_br, w_dn_br, w_o, g_ffn, w_up, ffn_conv_w, ffn_conv_b, w_down, g_final):
    inp = dict(x=np.asarray(x), g_mix=np.asarray(g_mix), w_in=np.asarray(w_in), ret_norm_g=np.asarray(ret_norm_g),
               dn_conv_w=np.asarray(dn_conv_w), dn_a_log=np.asarray(dn_a_log), dn_dt_bias=np.asarray(dn_dt_bias),
               dn_norm_g=np.asarray(dn_norm_g), w_ret_br=np.asarray(w_ret_br), w_dn_br=np.asarray(w_dn_br),
               w_o=np.asarray(w_o), g_ffn=np.asarray(g_ffn), w_up=np.asarray(w_up), ffn_conv_w=np.asarray(ffn_conv_w),
               ffn_conv_b=np.asarray(ffn_conv_b), w_down=np.asarray(w_down), g_final=np.asarray(g_final))
    B, T, _ = inp["x"].shape
    assert B == 4
    TB = T // 2
    ncA = build_A(T)
    mapsA = [stageA_inputs(inp, c // 2, c % 2, T) for c in range(8)]
    resA = run_bass_kernel_spmd(ncA, mapsA, core_ids=list(range(8)))
    ncB = build_B(TB)
    mapsB = []
    for c in range(8):
        b, r = c // 2, c % 2
        o0 = np.asarray(resA.results[2 * b]["ogA"])
        o1 = np.asarray(resA.results[2 * b + 1]["ogA"])
        og_full = np.concatenate([o0[0:512], o1[0:512], o0[512:1024], o1[512:1024]], axis=0)
        mapsB.append(stageB_inputs(inp, b, r, TB, og_full))
    resB = run_bass_kernel_spmd(ncB, mapsB, core_ids=list(range(8)))
    out = np.zeros((B, T, D), np.float32)
    for c in range(8):
        b, r = c // 2, c % 2
        out[b, r * TB:(r + 1) * TB] = np.asarray(resB.results[c]["out"])
    return out
```
